# Optimizing a Trainium2 kernel written in Bass

```python
import math
import jax
import jax.numpy as jnp
from jax import lax
import numpy as np

D_MODEL = 1024
BATCH = 2
SEQ = 16384
DEPTH = 2

MEM_LEN = 256
EPS = 1e-6
MASK_VALUE = -1e30
TINY = 1e-30

A_HEADS = 8
A_NOPE = 64
A_ROPE = 32
A_V = 64
A_Q_RANK = 384
A_KV_RANK = 256
A_QBLOCK = 128
ROPE_THETA = 10000.0

B_HEADS = 8
B_DK = 128
B_DV = 64
B_CHUNK = 16

C_HEADS = 8
C_KV_HEADS = 2
C_DH = 64
C_WINDOW = 128
C_BLOCK = 128

REL_BUCKETS = 32
REL_MAX_DIST = 128

X_HEADS = 4
X_DH = 256

D_FF = -(-(8 * D_MODEL) // (3 * 256)) * 256

IN_SPLITS = (
    A_Q_RANK, A_KV_RANK, A_ROPE,
    B_HEADS * B_DK, B_HEADS * B_DK, B_HEADS * B_DK,
    B_HEADS * B_DV, B_HEADS * B_DV,
    C_HEADS * C_DH, C_KV_HEADS * C_DH, C_KV_HEADS * C_DH,
    D_MODEL, D_MODEL, D_MODEL,
)
IN_WIDTH = sum(IN_SPLITS)

kernel_name = 'hybrid_mla_hgrn2_swa_encoder'


def _rmsnorm(x, g):
    x32 = x.astype(jnp.float32)
    y = x32 * lax.rsqrt(jnp.mean(x32 * x32, axis=-1, keepdims=True) + EPS)
    return (y * g.astype(jnp.float32)).astype(x.dtype)


def _split_cols(z, sizes):
    out, start = [], 0
    for s in sizes:
        out.append(z[..., start:start + s])
        start += s
    return out


def _rope(x, pos):
    half = x.shape[-1] // 2
    inv = ROPE_THETA ** (-jnp.arange(half, dtype=jnp.float32) / half)
    ang = pos.astype(jnp.float32)[:, None] * inv[None, :]
    cos = jnp.cos(ang)[None, :, None, :]
    sin = jnp.sin(ang)[None, :, None, :]
    x32 = x.astype(jnp.float32)
    x1, x2 = x32[..., :half], x32[..., half:]
    return jnp.concatenate([x1 * cos - x2 * sin, x1 * sin + x2 * cos], axis=-1).astype(x.dtype)


def _t5_bucket(rel):
    nb = REL_BUCKETS // 2
    max_exact = nb // 2
    ret = (rel > 0).astype(jnp.int32) * nb
    n = jnp.abs(rel)
    large = max_exact + (jnp.log(jnp.maximum(n, 1).astype(jnp.float32) / max_exact)
                         / math.log(REL_MAX_DIST / max_exact) * (nb - max_exact)).astype(jnp.int32)
    large = jnp.minimum(large, nb - 1)
    return ret + jnp.where(n < max_exact, n, large)


def _mla(cq, ckv, kr, gq, gkv, wuq, wukv, pos):
    Bsz, S, _ = cq.shape
    q = (_rmsnorm(cq, gq) @ wuq).reshape(Bsz, S, A_HEADS, A_NOPE + A_ROPE)
    q = jnp.concatenate([q[..., :A_NOPE], _rope(q[..., A_NOPE:], pos)], axis=-1)
    kv = (_rmsnorm(ckv, gkv) @ wukv).reshape(Bsz, S, A_HEADS, A_NOPE + A_V)
    k_rope = jnp.broadcast_to(_rope(kr[:, :, None, :], pos), (Bsz, S, A_HEADS, A_ROPE))
    k = jnp.concatenate([kv[..., :A_NOPE], k_rope], axis=-1)
    v = kv[..., A_NOPE:]
    scale = (A_NOPE + A_ROPE) ** -0.5
    nb = S // A_QBLOCK
    qb = jnp.moveaxis(q.reshape(Bsz, nb, A_QBLOCK, A_HEADS, A_NOPE + A_ROPE), 1, 0)

    def attend(q_blk):
        s = jnp.einsum('bqhd,bkhd->bhqk', q_blk, k).astype(jnp.float32) * scale
        p = jax.nn.softmax(s, axis=-1).astype(v.dtype)
        return jnp.einsum('bhqk,bkhd->bqhd', p, v)

    o = lax.map(attend, qb)
    return jnp.moveaxis(o, 0, 1).reshape(Bsz, S, A_HEADS * A_V)


def _gated_scan(q, k, v, log_f):
    Bsz, S, H, DK = q.shape
    DV = v.shape[-1]
    nc = S // B_CHUNK
    q, k, log_f = [t.reshape(Bsz, nc, B_CHUNK, H, DK) for t in (q, k, log_f)]
    v = v.reshape(Bsz, nc, B_CHUNK, H, DV)
    b = jnp.cumsum(log_f, axis=2)
    b_last = b[:, :, -1:]
    q_dec = q * jnp.exp(b)
    k_inv = k * jnp.exp(-b)
    k_end = k * jnp.exp(b_last - b)
    scores = jnp.einsum('bnthk,bnshk->bnhts', q_dec, k_inv)
    tri = jnp.tril(jnp.ones((B_CHUNK, B_CHUNK), dtype=bool))
    scores = jnp.where(tri, scores, 0.0)
    o_intra = jnp.einsum('bnhts,bnshv->bnthv', scores, v)

    def step(state, inp):
        q_c, k_c, v_c, dec_c = inp
        o_c = jnp.einsum('bthk,bhkv->bthv', q_c, state)
        state = state * dec_c[:, 0, :, :, None] + jnp.einsum('bshk,bshv->bhkv', k_c, v_c)
        return state, o_c

    xs = tuple(jnp.moveaxis(t, 1, 0) for t in (q_dec, k_end, v, jnp.exp(b_last)))
    s0 = jnp.zeros((Bsz, H, DK, DV), jnp.float32)
    _, o_inter = lax.scan(step, s0, xs)
    o = o_intra + jnp.moveaxis(o_inter, 0, 1)
    return o.reshape(Bsz, S, H, DV)


def _hgrn2(q, f_fwd, f_bwd, i, g, lb_fwd, lb_bwd, g_out):
    Bsz, S, _ = q.shape
    dt = q.dtype

    def heads(t, d):
        return t.astype(jnp.float32).reshape(Bsz, S, B_HEADS, d)

    def gates(z, lb):
        lb = lb.astype(jnp.float32).reshape(B_HEADS, B_DK)
        zh = heads(z, B_DK)
        f = lb + (1.0 - lb) * jax.nn.sigmoid(zh)
        log_f = jnp.log(jnp.maximum(f, TINY))
        key = (1.0 - lb) * jax.nn.sigmoid(-zh)
        return log_f, key

    qh = heads(q, B_DK)
    vh = heads(i, B_DV)
    lf_f, k_f = gates(f_fwd, lb_fwd)
    lf_b, k_b = gates(f_bwd, lb_bwd)
    o_f = _gated_scan(qh, k_f, vh, lf_f)
    flip = lambda t: jnp.flip(t, axis=1)
    o_b = flip(_gated_scan(flip(qh), flip(k_b), flip(vh), flip(lf_b)))
    o = _rmsnorm(o_f + o_b, g_out) * jax.nn.silu(heads(g, B_DV))
    return o.reshape(Bsz, S, B_HEADS * B_DV).astype(dt)


def _window_gqa(q, k, v, rel_bias, sink):
    Bsz, S, _ = q.shape
    nb = S // C_BLOCK
    G = C_HEADS // C_KV_HEADS
    span = 3 * C_BLOCK
    q = q.reshape(Bsz, nb, C_BLOCK, C_KV_HEADS, G, C_DH)

    def band(t):
        t = t.reshape(Bsz, S, C_KV_HEADS, C_DH)
        t = jnp.pad(t, ((0, 0), (C_BLOCK, C_BLOCK), (0, 0), (0, 0)))
        t = t.reshape(Bsz, nb + 2, C_BLOCK, C_KV_HEADS, C_DH)
        return jnp.concatenate([t[:, :-2], t[:, 1:-1], t[:, 2:]], axis=2)

    kb, vb = band(k), band(v)
    rel = jnp.arange(span)[None, :] - C_BLOCK - jnp.arange(C_BLOCK)[:, None]
    bias = rel_bias.astype(jnp.float32)[_t5_bucket(rel)]
    bias = jnp.transpose(bias, (2, 0, 1)).reshape(C_KV_HEADS, G, C_BLOCK, span)
    key_pos = (jnp.arange(nb)[:, None] - 1) * C_BLOCK + jnp.arange(span)[None, :]
    valid = (jnp.abs(rel) <= C_WINDOW)[None] & ((key_pos >= 0) & (key_pos < S))[:, None, :]
    s = jnp.einsum('bnqkgd,bnskd->bnkgqs', q, kb).astype(jnp.float32) * (C_DH ** -0.5) + bias
    s = jnp.where(valid[None, :, None, None], s, MASK_VALUE)
    sink_l = sink.astype(jnp.float32).reshape(C_KV_HEADS, G)[:, :, None, None]
    m = jnp.maximum(jnp.max(s, axis=-1, keepdims=True), sink_l)
    p = jnp.exp(s - m)
    p = p / (jnp.sum(p, axis=-1, keepdims=True) + jnp.exp(sink_l - m))
    o = jnp.einsum('bnkgqs,bnskd->bnqkgd', p.astype(v.dtype), vb)
    return o.reshape(Bsz, S, C_HEADS * C_DH)


def _cross(h, mem_n, wq, wkv, wo):
    Bsz, S, _ = h.shape
    q = (h @ wq).reshape(Bsz, S, X_HEADS, X_DH)
    kv = (mem_n @ wkv).reshape(Bsz, mem_n.shape[1], 2, X_HEADS, X_DH)
    s = jnp.einsum('bqhd,bkhd->bhqk', q, kv[:, :, 0]).astype(jnp.float32) * (X_DH ** -0.5)
    p = jax.nn.softmax(s, axis=-1).astype(h.dtype)
    o = jnp.einsum('bhqk,bkhd->bqhd', p, kv[:, :, 1]).reshape(Bsz, S, X_HEADS * X_DH)
    return o @ wo


def _swiglu(h, w1, w3, w2):
    return (jax.nn.silu(h @ w1) * (h @ w3)) @ w2


def setup_inputs(seed: int = 0) -> dict:
    key = jax.random.key(seed)
    ks = iter(jax.random.split(key, 32))
    f32 = jnp.float32

    def nrm(shape, fan_in):
        return jax.random.normal(next(ks), shape, f32) * (fan_in ** -0.5)

    def gain(shape):
        return 1.0 + 0.02 * jax.random.normal(next(ks), shape, f32)

    L, D = DEPTH, D_MODEL
    return {
        'x': jax.random.normal(next(ks), (BATCH, SEQ, D), f32),
        'mem': jax.random.normal(next(ks), (BATCH, MEM_LEN, D), f32),
        'w_in': nrm((L, D, IN_WIDTH), D),
        'g_mix': gain((L, D)),
        'a_gq': gain((L, A_Q_RANK)),
        'a_gkv': gain((L, A_KV_RANK)),
        'a_wuq': nrm((L, A_Q_RANK, A_HEADS * (A_NOPE + A_ROPE)), A_Q_RANK),
        'a_wukv': nrm((L, A_KV_RANK, A_HEADS * (A_NOPE + A_V)), A_KV_RANK),
        'b_lb': jax.random.normal(next(ks), (2, L, B_HEADS * B_DK), f32),
        'b_gout': gain((L, B_DV)),
        'c_sink': 0.5 * jax.random.normal(next(ks), (L, C_HEADS), f32),
        'rel_bias': 0.5 * jax.random.normal(next(ks), (REL_BUCKETS, C_HEADS), f32),
        'w_br_a': nrm((L, A_HEADS * A_V, D), A_HEADS * A_V),
        'w_br_b': nrm((L, B_HEADS * B_DV, D), B_HEADS * B_DV),
        'w_br_c': nrm((L, C_HEADS * C_DH, D), C_HEADS * C_DH),
        'w_out': nrm((L, D, D), D),
        'g_x': gain((L, D)),
        'g_mem': gain((L, D)),
        'x_wq': nrm((L, D, X_HEADS * X_DH), D),
        'x_wkv': nrm((L, D, 2 * X_HEADS * X_DH), D),
        'x_wo': nrm((L, X_HEADS * X_DH, D), X_HEADS * X_DH),
        'g_ffn': gain((L, D)),
        'f_w1': nrm((L, D, D_FF), D),
        'f_w3': nrm((L, D, D_FF), D),
        'f_w2': nrm((L, D_FF, D), D_FF),
        'g_final': gain((D,)),
    }


def reference(x, mem, w_in, g_mix, a_gq, a_gkv, a_wuq, a_wukv, b_lb, b_gout, c_sink, rel_bias,
              w_br_a, w_br_b, w_br_c, w_out, g_x, g_mem, x_wq, x_wkv, x_wo, g_ffn,
              f_w1, f_w3, f_w2, g_final):
    S = x.shape[1]
    pos = jnp.arange(S, dtype=jnp.int32)
    sm = jax.nn.softmax(b_lb.astype(jnp.float32), axis=1)
    lower_bounds = jnp.cumsum(sm, axis=1) - sm[:, :1]
    for l in range(DEPTH):
        h = _rmsnorm(x, g_mix[l])
        (a_cq, a_ckv, a_kr, b_q, b_ff, b_fb, b_i, b_g,
         c_q, c_k, c_v, gate_a, gate_b, gate_c) = _split_cols(h @ w_in[l], IN_SPLITS)
        y_a = _mla(a_cq, a_ckv, a_kr, a_gq[l], a_gkv[l], a_wuq[l], a_wukv[l], pos)
        y_b = _hgrn2(b_q, b_ff, b_fb, b_i, b_g, lower_bounds[0, l], lower_bounds[1, l], b_gout[l])
        y_c = _window_gqa(c_q, c_k, c_v, rel_bias, c_sink[l])
        merged = (jax.nn.sigmoid(gate_a) * (y_a @ w_br_a[l])
                  + jax.nn.sigmoid(gate_b) * (y_b @ w_br_b[l])
                  + jax.nn.sigmoid(gate_c) * (y_c @ w_br_c[l]))
        x = x + merged @ w_out[l]
        h = _rmsnorm(x, g_x[l])
        x = x + _cross(h, _rmsnorm(mem, g_mem[l]), x_wq[l], x_wkv[l], x_wo[l])
        h = _rmsnorm(x, g_ffn[l])
        x = x + _swiglu(h, f_w1[l], f_w3[l], f_w2[l])
    return _rmsnorm(x, g_final)
```

```python
import contextlib
import numpy as np
import ml_dtypes
import concourse.bass as bass
import concourse.mybir as mybir
from concourse.bass_utils import run_bass_kernel_spmd

F32 = mybir.dt.float32
BF16 = mybir.dt.bfloat16
ALU = mybir.AluOpType
AF = mybir.ActivationFunctionType

D = 1024
EPS = 1e-6
IN_W = 8608
O_CQ, O_CKV, O_KR = 0, 384, 640
O_BQ, O_BFF, O_BFB, O_BI, O_BG = 672, 1696, 2720, 3744, 4256
O_CQW, O_CK, O_CV = 4768, 5280, 5408
O_GA = 5536
DFF = 2816


class Sched:
    ENGS = ("pe", "act", "dve", "pool", "sp")

    def __init__(self, nc, stack, n_dma_sems=10):
        self.nc = nc
        self.eng = {"pe": nc.tensor, "act": nc.scalar, "dve": nc.vector,
                    "pool": nc.gpsimd, "sp": nc.sync}
        self.esem = {e: stack.enter_context(nc.semaphore("es_" + e)) for e in self.ENGS}
        self.ecount = {e: 0 for e in self.ENGS}
        self.dsem = {q: [[stack.enter_context(nc.semaphore(f"ds_{q}{i}")), 0]
                         for i in range(n_dma_sems)] for q in ("sp", "act", "pool")}
        self.dnext = {q: 0 for q in self.dsem}
        self.waited = {e: {} for e in self.ENGS}
        self.recs = []
        self.last_w = {}
        self.readers = {}
        self.ninst = 0

    def add(self, eng, fn, reads=(), writes=(), dma=False):
        idx = len(self.recs)
        deps = {}
        for k in reads:
            w = self.last_w.get(k)
            if w is not None:
                deps[w] = True
            if eng != "pe" and isinstance(k, tuple) and k[0] == "ps":
                for r in self.readers.get(k, ()):
                    if self.recs[r]["eng"] != eng:
                        deps.setdefault(r, False)
        for k in writes:
            w = self.last_w.get(k)
            if w is not None:
                deps.setdefault(w, False)
            for r in self.readers.get(k, ()):
                deps.setdefault(r, False)
        for k in writes:
            self.last_w[k] = idx
            self.readers[k] = []
        for k in reads:
            self.readers.setdefault(k, []).append(idx)
        keep = []
        for d, raw in deps.items():
            if d == idx:
                continue
            r = self.recs[d]
            if (not dma) and (not r["dma"]) and r["eng"] == eng:
                if eng == "pe":
                    continue
            keep.append(d)
        self.recs.append({"eng": eng, "fn": fn, "dma": dma, "deps": keep,
                          "target": False, "tok": None})
        return idx

    def dma(self, q, fn, reads=(), writes=()):
        return self.add(q, fn, reads, writes, dma=True)

    def _wait(self, eng, sem, val):
        w = self.waited[eng]
        key = id(sem)
        if w.get(key, 0) < val:
            self.eng[eng].wait_ge(sem, val)
            w[key] = val

    def flush(self):
        recs = self.recs
        last_of = {}
        for i, r in enumerate(recs):
            for d in r["deps"]:
                recs[d]["target"] = True
            if not r["dma"]:
                last_of[r["eng"]] = i
        for e, i in last_of.items():
            recs[i]["target"] = True
        for r in recs:
            eng = r["eng"]
            for d in sorted(r["deps"]):
                sem, val = recs[d]["tok"]
                self._wait(eng, sem, val)
            if r["dma"]:
                pool = self.dsem[eng]
                j = self.dnext[eng]
                self.dnext[eng] = (j + 1) % len(pool)
                sem, val = pool[j]
                if val:
                    self._wait(eng, sem, val)
                inst = r["fn"]()
                inst.then_inc(sem, 16)
                pool[j][1] = val + 16
                r["tok"] = (sem, val + 16)
            else:
                inst = r["fn"]()
                if r["target"]:
                    inst.then_inc(self.esem[eng], 1)
                    self.ecount[eng] += 1
                r["tok"] = (self.esem[eng], self.ecount[eng])
            self.ninst += 1
            r["fn"] = None
        for e in self.ENGS:
            for e2 in self.ENGS:
                if e2 != e and self.ecount[e2]:
                    self._wait(e, self.esem[e2], self.ecount[e2])
            for q in self.dsem:
                for sem, val in self.dsem[q]:
                    if val:
                        self._wait(e, sem, val)
        self.last_w = {}
        self.readers = {}
        self.recs = []


class B:
    def __init__(self, nc, S):
        self.nc, self.S = nc, S

    def mm(self, out, lhsT, rhs, start, stop, r=(), w=()):
        nc = self.nc
        self.S.add("pe", lambda: nc.tensor.matmul(out, lhsT=lhsT, rhs=rhs, start=start, stop=stop), r, w)

    def tr(self, out, in_, ident, r=(), w=()):
        nc = self.nc
        self.S.add("pe", lambda: nc.tensor.transpose(out, in_, ident), r, w)

    def act(self, out, in_, func, r=(), w=(), **kw):
        nc = self.nc
        self.S.add("act", lambda: nc.scalar.activation(out=out, in_=in_, func=func, **kw), r, w)

    def tt(self, out, in0, in1, op, r=(), w=(), eng="dve"):
        e = self.S.eng[eng]
        self.S.add(eng, lambda: e.tensor_tensor(out=out, in0=in0, in1=in1, op=op), r, w)

    def ts(self, out, in0, s1, s2, op0, op1=None, r=(), w=(), eng="dve"):
        e = self.S.eng[eng]
        if op1 is None:
            self.S.add(eng, lambda: e.tensor_scalar(out=out, in0=in0, scalar1=s1, scalar2=None, op0=op0), r, w)
        else:
            self.S.add(eng, lambda: e.tensor_scalar(out=out, in0=in0, scalar1=s1, scalar2=s2, op0=op0, op1=op1), r, w)

    def stt(self, out, in0, scalar, in1, op0, op1, r=(), w=()):
        nc = self.nc
        self.S.add("dve", lambda: nc.vector.scalar_tensor_tensor(out=out, in0=in0, scalar=scalar, in1=in1, op0=op0, op1=op1), r, w)

    def cp(self, out, in_, r=(), w=(), eng="dve"):
        e = self.S.eng[eng]
        if eng == "act":
            self.S.add(eng, lambda: e.copy(out=out, in_=in_), r, w)
        else:
            self.S.add(eng, lambda: e.tensor_copy(out=out, in_=in_), r, w)

    def recip(self, out, in_, r=(), w=()):
        nc = self.nc
        self.S.add("dve", lambda: nc.vector.reciprocal(out=out, in_=in_), r, w)

    def memset(self, ap, val, w=(), eng="pool"):
        e = self.S.eng[eng]
        self.S.add(eng, lambda: e.memset(ap, val), (), w)

    def dma(self, out, in_, r=(), w=(), q="sp"):
        e = self.S.eng[q]
        self.S.dma(q, lambda: e.dma_start(out=out, in_=in_), r, w)


def dram_in(nc, name, shape, dt=F32):
    return nc.dram_tensor(name, list(shape), dt, kind="ExternalInput").ap()


def dram_out(nc, name, shape, dt=F32):
    return nc.dram_tensor(name, list(shape), dt, kind="ExternalOutput").ap()


def phase_inproj(nc, S, b, T, dr, st):
    TB = 512
    NTB = T // TB
    NT = T // 128
    sb = lambda name, shape, dt: st.enter_context(nc.sbuf_tensor("s_" + name, shape, dt))
    hT = sb("hT", [128, 8, T], BF16)
    ident = sb("ident", [128, 128], BF16)
    ones = sb("ones", [128, 128], BF16)
    gbc = sb("gbc", [128, D], F32)
    xb = [sb(f"xb{i}", [128, D], F32) for i in range(2)]
    hb = [sb(f"hb{i}", [128, D], BF16) for i in range(2)]
    junk = sb("junk", [128, D], BF16)
    ssq = [sb(f"ssq{i}", [128, 1], F32) for i in range(2)]
    rs1 = [sb(f"rs1{i}", [128, 1], F32) for i in range(2)]
    rs2 = [sb(f"rs2{i}", [128, 1], F32) for i in range(2)]
    ps = [st.enter_context(nc.psum_tensor(f"ps{i}", [128, 512], F32)) for i in range(7)]
    psT = st.enter_context(nc.psum_tensor("psT", [128, 8, 128], BF16))

    b.dma(ident[:], dr["ident"], w=["ident"])
    b.dma(ones[:], dr["ones"], w=["ones"])
    b.dma(gbc[:], dr["g_mix"].partition_broadcast(128), w=["gbc"])

    for tt in range(NT):
        i = tt % 2
        b.dma(xb[i][:], dr["x"][tt * 128:(tt + 1) * 128, :], w=[("xb", i)])
        b.act(junk[:], xb[i][:], AF.Square, r=[("xb", i)], w=["junk", ("ssq", i)], accum_out=ssq[i][:])
        b.act(rs1[i][:], ssq[i][:], AF.Sqrt, r=[("ssq", i)], w=[("rs1", i)], scale=1.0 / D, bias=EPS)
        b.recip(rs2[i][:], rs1[i][:], r=[("rs1", i)], w=[("rs2", i)])
        b.stt(hb[i][:], xb[i][:], rs2[i][:, 0:1], gbc[:], ALU.mult, ALU.mult,
              r=[("xb", i), ("rs2", i), "gbc"], w=[("hb", i)])
        for j in range(8):
            b.tr(psT[:, j, :], hb[i][:, j * 128:(j + 1) * 128], ident[:], r=[("hb", i), "ident"], w=["psT"])
        b.cp(hT[:, :, tt * 128:(tt + 1) * 128], psT[:, :, :], r=["psT"], w=[("hT", tt // 4)],
             eng="act" if tt % 2 else "dve")

    import os
    STOP = int(os.environ.get("STOP", "0"))
    if STOP == 1:
        return
    WB = 3
    wb = [sb(f"wb{i}", [128, 8, 768], BF16) for i in range(WB)]
    wuq = sb("wuq", [128, 3, 768], BF16)
    wuqs = sb("wuqs", [128, 3, 768], BF16)
    wuk = sb("wuk", [128, 2, 512], BF16)
    wuv = sb("wuv", [128, 2, 512], BF16)
    gq = sb("gq", [128, 3], F32)
    gkv = sb("gkv", [128, 2], F32)
    w_r = lambda ap, kc: ap.rearrange("(kc p) n -> p kc n", p=128)
    b.dma(wuq[:], w_r(dr["a_wuq"], 3), w=["wuq"], q="pool")
    b.dma(wuqs[:], w_r(dr["a_wuq_sw"], 3), w=["wuqs"], q="pool")
    b.dma(wuk[:], w_r(dr["a_wuk"], 2), w=["wuk"], q="pool")
    b.dma(wuv[:], w_r(dr["a_wuv"], 2), w=["wuv"], q="pool")
    b.dma(gq[:], dr["a_gq"], w=["gq"])
    b.dma(gkv[:], dr["a_gkv"], w=["gkv"])
    blb = sb("blb", [128, 2, 2, 8], F32)
    eb = sb("eb", [128, 2, 2, 8], F32)
    lmask = sb("lmask", [128, 2], F32)
    lbt = sb("lbt", [128, 2, 8], F32)
    oml = sb("oml", [128, 2, 8], F32)
    noml = sb("noml", [128, 2, 8], F32)
    den = sb("den", [128, 2, 8], F32)
    b.dma(blb[:], dr["b_lb"], w=["blb"])
    b.dma(lmask[:], dr["lmask"], w=["lmask"])
    b.act(eb[:], blb[:], AF.Exp, r=["blb"], w=["eb"])
    b.tt(den[:], eb[:, :, 0, :], eb[:, :, 1, :], ALU.add, r=["eb"], w=["den"])
    b.S.add("dve", lambda: nc.vector.reciprocal(out=den[:], in_=den[:]), ["den"], ["den"])
    for l in range(2):
        b.ts(eb[:, :, l, :], eb[:, :, l, :], lmask[:, l:l + 1], None, ALU.mult, r=["eb", "lmask"], w=["eb"])
    b.tt(lbt[:], eb[:, :, 0, :], eb[:, :, 1, :], ALU.add, r=["eb"], w=["lbt"])
    b.tt(lbt[:], lbt[:], den[:], ALU.mult, r=["lbt", "den"], w=["lbt"])
    b.ts(oml[:], lbt[:], -1.0, 1.0, ALU.mult, ALU.add, r=["lbt"], w=["oml"])
    b.ts(noml[:], oml[:], -1.0, None, ALU.mult, r=["oml"], w=["noml"])

    if STOP == 2:
        return
    sq = [sb(f"sq{i}", [128, TB], BF16) for i in range(3)]
    raw = [sb(f"raw{i}", [128, TB], F32) for i in range(3)]
    rst = sb("rst", [128, TB], F32)
    rst2 = sb("rst2", [128, TB], F32)
    cqn = [sb(f"cqn{i}", [128, TB], BF16) for i in range(3)]
    tabC = [sb(f"tabC{i}", [128, TB], F32) for i in range(2)]
    tabS = [sb(f"tabS{i}", [128, TB], F32) for i in range(2)]
    t1 = sb("t1", [128, TB], F32)
    t2 = sb("t2", [128, TB], F32)
    ob = [sb(f"ob{i}", [128, TB], BF16) for i in range(4)]
    of = [sb(f"of{i}", [128, TB], F32) for i in range(4)]
    sgm = [sb(f"sgm{i}", [128, TB], F32) for i in range(2)]
    cnt = {"ps": 0, "ob": 0, "of": 0, "wb": 0, "sg": 0}

    def nxt(k, n):
        v = cnt[k] % n
        cnt[k] += 1
        return v

    def load_w(col_lo, ncols, extra=None):
        s = nxt("wb", WB)
        b.dma(wb[s][:, :, 0:ncols], dr["w_in"][:, col_lo:col_lo + ncols].rearrange("(kc p) n -> p kc n", p=128),
              w=[("wb", s)], q="pool")
        if extra is not None:
            b.dma(wb[s][:, :, ncols:ncols + 32], extra.rearrange("(kc p) n -> p kc n", p=128), w=[("wb", s)], q="pool")
        return s

    def gemm_fm(bank, s, c0, ncol, tb):
        for kc in range(8):
            b.mm(ps[bank][0:ncol, :], wb[s][:, kc, c0:c0 + ncol], hT[:, kc, tb * TB:(tb + 1) * TB],
                 kc == 0, kc == 7, r=[("wb", s), ("hT", tb)], w=[("ps", bank)])

    def gemm_tm(bank, s, c0, ncol, tb, sub):
        t0 = tb * TB + sub * 128
        for kc in range(8):
            b.mm(ps[bank][:, 0:ncol], hT[:, kc, t0:t0 + 128], wb[s][:, kc, c0:c0 + ncol],
                 kc == 0, kc == 7, r=[("wb", s), ("hT", tb)], w=[("ps", bank)])

    def store(dst, src, key):
        b.dma(dst, src, r=[key])

    s = load_w(0, 672, extra=dr["w_kr_sw"])
    if STOP == 21:
        return
    for tb in range(NTB):
        tsl = slice(tb * TB, (tb + 1) * TB)
        ti = tb % 2
        for tab, name in ((tabC, "ropeC"), (tabS, "ropeS")):
            if STOP in (23, 24, 25, 26):
                continue
            b.dma(tab[ti][0:32, :], dr[name][:, tsl], w=[(name, ti)])
            b.dma(tab[ti][64:96, :], dr[name][:, tsl], w=[(name, ti)])
        for (base, nch, n, gvec, gname, kind) in ((0, 3, 384, gq, "gq", "q"), (384, 2, 256, gkv, "gkv", "kv")):
            for c in range(nch):
                gemm_fm(c, s, base + c * 128, 128, tb)
                if STOP == 24:
                    continue
                b.cp(raw[c][:], ps[c][:], r=[("ps", c)], w=[("raw", c)])
                b.act(sq[c][:], raw[c][:], AF.Square, r=[("raw", c)], w=[("sq", c)])
            if STOP in (22, 23, 24, 25, 26):
                continue
            for c in range(nch):
                b.mm(ps[4][:], ones[:], sq[c][:], c == 0, c == nch - 1, r=["ones", ("sq", c)], w=[("ps", 4)])
            b.act(rst[:], ps[4][:], AF.Sqrt, r=[("ps", 4)], w=["rst"], scale=1.0 / n, bias=EPS)
            b.recip(rst2[:], rst[:], r=["rst"], w=["rst2"])
            for c in range(nch):
                b.stt(cqn[c][:], raw[c][:], gvec[:, c:c + 1], rst2[:], ALU.mult, ALU.mult,
                      r=[("raw", c), gname, "rst2"], w=[("cqn", c)])
            if STOP == 31:
                continue
            if kind == "kv" and STOP == 32:
                continue
            if kind == "q":
                for h in range(8):
                    for (wq, bank, wn) in ((wuq, 5, "wuq"), (wuqs, 6, "wuqs")):
                        for c in range(3):
                            b.mm(ps[bank][0:96, :], wq[:, c, h * 96:(h + 1) * 96], cqn[c][:], c == 0, c == 2,
                                 r=[wn, ("cqn", c)], w=[("ps", bank)])
                    o = nxt("ob", 4)
                    b.cp(ob[o][0:64, :], ps[5][0:64, :], r=[("ps", 5)], w=[("ob", o)], eng="dve")
                    b.tt(t1[64:96, :], ps[5][64:96, :], tabC[ti][64:96, :], ALU.mult, r=[("ps", 5), ("ropeC", ti)], w=["t1"])
                    b.tt(t2[64:96, :], ps[6][64:96, :], tabS[ti][64:96, :], ALU.mult, r=[("ps", 6), ("ropeS", ti)], w=["t2"])
                    b.tt(ob[o][64:96, :], t1[64:96, :], t2[64:96, :], ALU.add, r=["t1", "t2"], w=[("ob", o)])
                    store(dr["qa"][h, :, tsl], ob[o][0:96, :], ("ob", o))
            else:
                for h in range(8):
                    for c in range(2):
                        b.mm(ps[5][0:64, :], wuk[:, c, h * 64:(h + 1) * 64], cqn[c][:], c == 0, c == 1,
                             r=["wuk", ("cqn", c)], w=[("ps", 5)])
                    o = nxt("ob", 4)
                    b.cp(ob[o][0:64, :], ps[5][0:64, :], r=[("ps", 5)], w=[("ob", o)], eng="act" if h % 2 else "dve")
                    store(dr["ka"][h, 0:64, tsl], ob[o][0:64, :], ("ob", o))
                for sub in range(4):
                    for c in range(2):
                        b.mm(ps[6][:, :], cqn[c][:, sub * 128:(sub + 1) * 128], wuv[:, c, :], c == 0, c == 1,
                             r=["wuv", ("cqn", c)], w=[("ps", 6)])
                    o = nxt("ob", 4)
                    b.cp(ob[o][:, :], ps[6][:, :], r=[("ps", 6)], w=[("ob", o)], eng="act" if sub % 2 else "dve")
                    store(dr["va"][tb * TB + sub * 128: tb * TB + (sub + 1) * 128, :], ob[o][:, :], ("ob", o))
        if STOP in (22, 23, 24, 25, 26, 31, 32, 33):
            continue
        gemm_fm(0, s, 640, 32, tb)
        gemm_fm(1, s, 672, 32, tb)
        b.tt(t1[0:32, :], ps[0][0:32, :], tabC[ti][0:32, :], ALU.mult, r=[("ps", 0), ("ropeC", ti)], w=["t1"])
        b.tt(t2[0:32, :], ps[1][0:32, :], tabS[ti][0:32, :], ALU.mult, r=[("ps", 1), ("ropeS", ti)], w=["t2"])
        o = nxt("ob", 4)
        b.tt(ob[o][0:32, :], t1[0:32, :], t2[0:32, :], ALU.add, r=["t1", "t2"], w=[("ob", o)])
        for h in range(8):
            store(dr["ka"][h, 64:96, tsl], ob[o][0:32, :], ("ob", o))

    if STOP in (3, 22, 23, 24, 25, 26, 31, 32, 33):
        return
    def fm_group(col_lo, nchunk, epi):
        s = load_w(col_lo, nchunk * 128)
        for tb in range(NTB):
            for c in range(nchunk):
                bank = nxt("ps", 4)
                gemm_fm(bank, s, c * 128, 128, tb)
                epi(bank, c, tb)

    def tm_group(col_lo, ncol, dst, scale=None):
        s = load_w(col_lo, ncol)
        for tb in range(NTB):
            for sub in range(4):
                bank = 4 + nxt("ps", 2)
                gemm_tm(bank, s, 0, ncol, tb, sub)
                o = nxt("ob", 4)
                b.cp(ob[o][:, 0:ncol], ps[bank][:, 0:ncol], r=[("ps", bank)], w=[("ob", o)],
                     eng="act" if sub % 2 else "dve")
                store(dst[tb * TB + sub * 128: tb * TB + (sub + 1) * 128, :], ob[o][:, 0:ncol], ("ob", o))

    def epi_copy(dst_fn, scale=None):
        def epi(bank, c, tb):
            o = nxt("ob", 4)
            if scale is None:
                b.cp(ob[o][:], ps[bank][:], r=[("ps", bank)], w=[("ob", o)], eng="act" if o % 2 else "dve")
            else:
                b.act(ob[o][:], ps[bank][:], AF.Copy, r=[("ps", bank)], w=[("ob", o)], scale=scale)
            store(dst_fn(c)[:, tb * TB:(tb + 1) * TB], ob[o][:], ("ob", o))
        return epi

    def epi_func(dst_fn, func):
        def epi(bank, c, tb):
            o = nxt("ob", 4)
            b.act(ob[o][:], ps[bank][:], func, r=[("ps", bank)], w=[("ob", o)])
            store(dst_fn(c)[:, tb * TB:(tb + 1) * TB], ob[o][:], ("ob", o))
        return epi

    def epi_forget(d, h0):
        def epi(bank, c, tb):
            h = h0 + c
            g = nxt("sg", 2)
            b.act(sgm[g][:], ps[bank][:], AF.Sigmoid, r=[("ps", bank)], w=[("sgm", g)])
            o1 = nxt("of", 4)
            b.act(of[o1][:], sgm[g][:], AF.Ln, r=[("sgm", g), "oml", "lbt"], w=[("of", o1)],
                  scale=oml[:, d, h:h + 1], bias=lbt[:, d, h:h + 1])
            store(dr["lf"][d, h, :, tb * TB:(tb + 1) * TB], of[o1][:], ("of", o1))
            o2 = nxt("of", 4)
            b.ts(of[o2][:], sgm[g][:], noml[:, d, h:h + 1], oml[:, d, h:h + 1], ALU.mult, ALU.add,
                 r=[("sgm", g), "oml", "noml"], w=[("of", o2)])
            store(dr["ky"][d, h, :, tb * TB:(tb + 1) * TB], of[o2][:], ("of", o2))
        return epi

    for g in range(2):
        fm_group(O_BQ + g * 512, 4, epi_copy(lambda c, g=g: dr["qb"][g * 4 + c]))
    for d, off in ((0, O_BFF), (1, O_BFB)):
        for g in range(2):
            fm_group(off + g * 512, 4, epi_forget(d, g * 4))
    tm_group(O_BI, 512, dr["vb"])
    fm_group(O_BG, 4, epi_func(lambda c: dr["gb"][c * 128:(c + 1) * 128], AF.Silu))
    fm_group(O_CQW, 4, epi_copy(lambda c: dr["qc"][c * 128:(c + 1) * 128], scale=0.125))
    fm_group(O_CK, 1, epi_copy(lambda c: dr["kc"]))
    tm_group(O_CV, 128, dr["vc"])
    for g in range(6):
        fm_group(O_GA + g * 512, 4, epi_func(lambda c, g=g: dr["sg"][(g * 4 + c) * 128:(g * 4 + c + 1) * 128], AF.Sigmoid))


A_INPUTS = {
    "x": ("T", D), "w_in": (D, IN_W), "w_kr_sw": (D, 32), "g_mix": (D,), "a_gq": (128, 3), "a_gkv": (128, 2),
    "a_wuq": (384, 768), "a_wuq_sw": (384, 768), "a_wuk": (256, 512), "a_wuv": (256, 512),
    "b_lb": (128, 2, 2, 8), "lmask": (128, 2), "ropeC": (32, "T"), "ropeS": (32, "T"),
    "ones": (128, 128),
}
A_OUTPUTS = {
    "qa": ((8, 96, "T"), BF16), "ka": ((8, 96, "T"), BF16), "va": (("T", 512), BF16),
    "qb": ((8, 128, "T"), BF16), "lf": ((2, 8, 128, "T"), F32), "ky": ((2, 8, 128, "T"), F32),
    "vb": (("T", 512), BF16), "gb": ((512, "T"), BF16), "qc": ((512, "T"), BF16), "kc": ((128, "T"), BF16),
    "vc": (("T", 128), BF16), "sg": ((3072, "T"), BF16),
}


def _shape(s, T):
    return [T if v == "T" else v for v in s]


def build_A(T):
    nc = bass.Bass("TRN2", target_bir_lowering=False)
    dr = {}
    for k, s in A_INPUTS.items():
        dr[k] = dram_in(nc, k, _shape(s, T), BF16 if k == "ones" else F32)
    dr["ident"] = dram_in(nc, "ident", [128, 128], BF16)
    for k, (s, dt) in A_OUTPUTS.items():
        dr[k] = dram_out(nc, k, _shape(s, T), dt)
    with contextlib.ExitStack() as st:
        S = Sched(nc, st)
        b = B(nc, S)
        with contextlib.ExitStack() as st2:
            phase_inproj(nc, S, b, T, dr, st2)
            S.flush()
    return nc


def rope_tables(pos):
    half = 16
    inv = (10000.0 ** (-np.arange(half, dtype=np.float32) / half)).astype(np.float32)
    ang = pos.astype(np.float32)[None, :] * inv[:, None]
    c, s = np.cos(ang).astype(np.float32), np.sin(ang).astype(np.float32)
    return np.concatenate([c, c], 0), np.concatenate([-s, s], 0)


def layer_consts_A(inp, l):
    w_in = np.ascontiguousarray(inp["w_in"][l])
    wuq = inp["a_wuq"][l].reshape(384, 8, 96)
    wuq_sw = np.concatenate([wuq[:, :, :64], wuq[:, :, 80:96], wuq[:, :, 64:80]], axis=2)
    wukv = inp["a_wukv"][l].reshape(256, 8, 128)
    blb = inp["b_lb"].reshape(2, 2, 8, 128).transpose(3, 0, 1, 2)
    lmask = np.zeros((128, 2), np.float32)
    lmask[:, 1:l + 1] = 1.0
    kr = w_in[:, O_KR:O_KR + 32]
    return {
        "w_in": w_in, "w_kr_sw": np.ascontiguousarray(np.concatenate([kr[:, 16:], kr[:, :16]], 1)),
        "g_mix": inp["g_mix"][l], "a_gq": np.ascontiguousarray(inp["a_gq"][l].reshape(3, 128).T),
        "a_gkv": np.ascontiguousarray(inp["a_gkv"][l].reshape(2, 128).T),
        "a_wuq": np.ascontiguousarray(wuq.reshape(384, 768)), "a_wuq_sw": np.ascontiguousarray(wuq_sw.reshape(384, 768)),
        "a_wuk": np.ascontiguousarray(wukv[:, :, :64].reshape(256, 512)),
        "a_wuv": np.ascontiguousarray(wukv[:, :, 64:].reshape(256, 512)),
        "b_lb": np.ascontiguousarray(blb), "lmask": lmask,
        "ones": np.ones((128, 128), np.float32).astype(ml_dtypes.bfloat16),
        "ident": np.eye(128, dtype=np.float32).astype(ml_dtypes.bfloat16),
    }


def phase_mla(nc, S, b, T, SEQ, dr, st):
    sb = lambda name, shape, dt: st.enter_context(nc.sbuf_tensor("m_" + name, shape, dt))
    NKT = SEQ // 128
    NQB = T // 512
    scale = 96 ** -0.5
    kT = [sb(f"kT{i}", [96, SEQ], BF16) for i in range(2)]
    vA = [sb(f"vA{i}", [128, NKT, 128], BF16) for i in range(2)]
    qT = [sb(f"qT{i}", [96, 512], BF16) for i in range(2)]
    NP = 4
    pT = [sb(f"pT{i}", [128, 512], BF16) for i in range(NP)]
    rec = sb("rec", [128, 512], F32)
    yo = [sb(f"yo{i}", [64, 512], BF16) for i in range(2)]
    sps = [st.enter_context(nc.psum_tensor(f"sps{i}", [128, 512], F32)) for i in range(NP)]
    ops = [st.enter_context(nc.psum_tensor(f"ops{i}", [128, 512], F32)) for i in range(2)]
    for i in range(2):
        b.memset(vA[i][:, :, 64:128], 1.0, w=[("vA", i)])
    n = 0
    qn = 0
    for h in range(8):
        hb = h % 2
        b.dma(kT[hb][:, :], dr["kaF"][h], w=[("kT", hb)])
        b.dma(vA[hb][:, :, 0:64], dr["vaF"][:, h * 64:(h + 1) * 64].rearrange("(t p) d -> p t d", p=128),
              w=[("vA", hb)], q="act")
        for qb in range(NQB):
            qi = qn % 2
            qn += 1
            b.dma(qT[qi][:, :], dr["qa"][h, :, qb * 512:(qb + 1) * 512], w=[("qT", qi)])
            ob = qn % 2

            def s_mm(kt):
                sl = (n + kt) % NP
                b.mm(sps[sl][:, :], kT[hb][:, kt * 128:(kt + 1) * 128], qT[qi][:, :], True, True,
                     r=[("kT", hb), ("qT", qi)], w=[("ps", "s", sl)])

            s_mm(0)
            if NKT > 1:
                s_mm(1)
            for kt in range(NKT):
                sl = (n + kt) % NP
                if kt + 2 < NKT:
                    s_mm(kt + 2)
                b.act(pT[sl][:, :], sps[sl][:, :], AF.Exp, r=[("ps", "s", sl)], w=[("pT", sl)], scale=scale)
                b.mm(ops[ob][:, :], vA[hb][:, kt, :], pT[sl][:, :], kt == 0, kt == NKT - 1,
                     r=[("vA", hb), ("pT", sl)], w=[("ps", "o", ob)])
            n += NKT
            b.recip(rec[64:128, :], ops[ob][64:128, :], r=[("ps", "o", ob)], w=["rec"])
            b.tt(yo[ob][:, :], ops[ob][0:64, :], rec[64:128, :], ALU.mult, r=[("ps", "o", ob), "rec"], w=[("yo", ob)])
            b.dma(dr["ya"][h * 64:(h + 1) * 64, qb * 512:(qb + 1) * 512], yo[ob][:, :], r=[("yo", ob)], q="pool")


def phase_win(nc, S, b, T, dr, st):
    sb = lambda name, shape, dt: st.enter_context(nc.sbuf_tensor("w_" + name, shape, dt))
    NB = T // 128
    qS = sb("qS", [128, 4, T], BF16)
    kH = sb("kH", [128, T + 256], BF16)
    vH = sb("vH", [128, NB + 2, 2, 128], BF16)
    bias = sb("bias", [128, 3, 8, 128], F32)
    expb = sb("expb", [128, 3, 8, 128], F32)
    expL = sb("expL", [128, 8, 128], F32)
    expR = sb("expR", [128, 8, 128], F32)
    flg = sb("flg", [128, 2], F32)
    snk = sb("snk", [128, 8], F32)
    esnk = sb("esnk", [128, 8], F32)
    ee = [sb(f"ee{i}", [128, 512], F32) for i in range(3)]
    pp = [sb(f"pp{i}", [128, 512], BF16) for i in range(3)]
    den = sb("den", [128, 512], F32)
    yo = [sb(f"yo{i}", [64, 512], BF16) for i in range(2)]
    sps = [st.enter_context(nc.psum_tensor(f"wsps{i}", [128, 512], F32)) for i in range(3)]
    ops = [st.enter_context(nc.psum_tensor(f"wops{i}", [128, 512], F32)) for i in range(2)]
    for g in range(2):
        b.dma(qS[g * 64:(g + 1) * 64, :, :], dr["qc"][g * 256:(g + 1) * 256, :].rearrange("(hh d) t -> d hh t", d=64),
              w=["qS"])
    b.dma(kH[:, :], dr["kcH"], w=["kH"])
    b.memset(vH[:, :, :, 64:128], 1.0, w=["vH"])
    for g in range(2):
        b.dma(vH[:, :, g, 0:64], dr["vcH"][:, g * 64:(g + 1) * 64].rearrange("(t p) d -> p t d", p=128), w=["vH"], q="act")
    b.dma(bias[:], dr["wbias"], w=["bias"])
    b.dma(flg[:], dr["wflag"], w=["flg"])
    b.dma(snk[:], dr["c_sink"], w=["snk"])
    b.act(expb[:], bias[:], AF.Exp, r=["bias"], w=["expb"])
    b.act(esnk[:], snk[:], AF.Exp, r=["snk"], w=["esnk"])
    b.ts(expL[:], expb[:, 0, :, :], flg[:, 0:1], None, ALU.mult, r=["expb", "flg"], w=["expL"])
    b.ts(expR[:], expb[:, 2, :, :], flg[:, 1:2], None, ALU.mult, r=["expb", "flg"], w=["expR"])
    n = 0
    for j in range(NB):
        for g in range(2):
            ob = n % 2
            n += 1
            for kb in range(3):
                b.mm(sps[kb][:, :], kH[g * 64:(g + 1) * 64, (j + kb) * 128:(j + kb + 1) * 128],
                     qS[g * 64:(g + 1) * 64, :, j * 128:(j + 1) * 128], True, True, r=["kH", "qS"], w=[("ps", "ws", kb)])
                b.act(ee[kb][:, :], sps[kb][:, :], AF.Exp, r=[("ps", "ws", kb)], w=[("ee", kb)])
                if kb == 0 and j == 0:
                    eb_ = expL[:, g * 4:(g + 1) * 4, :]
                elif kb == 2 and j == NB - 1:
                    eb_ = expR[:, g * 4:(g + 1) * 4, :]
                else:
                    eb_ = expb[:, kb, g * 4:(g + 1) * 4, :]
                b.tt(pp[kb][:, :].rearrange("p (h q) -> p h q", h=4), ee[kb][:, :].rearrange("p (h q) -> p h q", h=4), eb_,
                     ALU.mult, r=[("ee", kb), "expb", "expL", "expR"], w=[("pp", kb)], eng="pool")
                b.mm(ops[ob][:, :], vH[:, j + kb, g, :], pp[kb][:, :], kb == 0, kb == 2, r=["vH", ("pp", kb)], w=[("ps", "wo", ob)])
            b.tt(den[64:128, :].rearrange("p (h q) -> p h q", h=4), ops[ob][64:128, :].rearrange("p (h q) -> p h q", h=4),
                 esnk[64:128, g * 4:(g + 1) * 4].unsqueeze(2).to_broadcast([64, 4, 128]), ALU.add,
                 r=[("ps", "wo", ob), "esnk"], w=["den"])
            b.recip(den[64:128, :], den[64:128, :], r=["den"], w=["den"])
            b.tt(yo[ob][:, :], ops[ob][0:64, :], den[64:128, :], ALU.mult, r=[("ps", "wo", ob), "den"], w=[("yo", ob)])
            b.dma(dr["yc"][g * 256:(g + 1) * 256, j * 128:(j + 1) * 128].rearrange("(hh d) t -> d hh t", d=64),
                  yo[ob][:, :].rearrange("p (h q) -> p h q", h=4), r=[("yo", ob)], q="pool")


def t5_bucket_np(rel):
    nb = 16
    max_exact = 8
    ret = (rel > 0).astype(np.int32) * nb
    n = np.abs(rel)
    large = max_exact + (np.log(np.maximum(n, 1).astype(np.float32) / max_exact)
                         / np.float32(np.log(128 / max_exact)) * (nb - max_exact)).astype(np.int32)
    large = np.minimum(large, nb - 1)
    return ret + np.where(n < max_exact, n, large)


def win_bias_table(rel_bias):
    key = np.arange(128)[:, None, None]
    kb = np.arange(3)[None, :, None]
    q = np.arange(128)[None, None, :]
    rel = kb * 128 + key - 128 - q
    bucket = t5_bucket_np(rel)
    tab = rel_bias[bucket]
    tab = np.where((np.abs(rel) <= 128)[..., None], tab, np.float32(-30000.0))
    return np.ascontiguousarray(np.transpose(tab, (0, 1, 3, 2))).astype(np.float32)


def phase_hgrn1(nc, S, b, T, dr, st):
    sb = lambda name, shape, dt: st.enter_context(nc.sbuf_tensor("g_" + name, shape, dt))
    NB = T // 128
    NC = T // 16
    lf = [sb(f"lf{i}", [128, T], F32) for i in range(2)]
    ky = [sb(f"ky{i}", [128, T], F32) for i in range(2)]
    qq = [sb(f"qq{i}", [128, T], BF16) for i in range(2)]
    bb = sb("bb", [128, T], F32)
    e1 = sb("e1", [128, T], F32)
    e2 = sb("e2", [128, T], F32)
    qd = sb("qd", [128, T], BF16)
    ki = sb("ki", [128, T], BF16)
    ke = sb("ke", [128, T], BF16)
    qp = sb("qp", [128, T], BF16)
    vv = [sb(f"vv{i}", [128, NB, 64], BF16) for i in range(2)]
    osb = sb("osb", [64, T], F32)
    smask = sb("smask", [128, 512], F32)
    tmask = sb("tmask", [128, 2, 128], F32)
    cmask = sb("cmask", [128, 8], BF16)
    ident = sb("ident", [128, 128], BF16)
    cum = sb("cum", [128, NC], F32)
    pex = sb("pex", [128, NC], F32)
    dec = sb("dec", [128, NC], F32)
    onesc = sb("onesc", [128, NC], F32)
    dtot = sb("dtot", [128, 2, 8], F32)
    stf = sb("stf", [128, 2, 8, 64], F32)
    st16 = [sb(f"st16{i}", [128, 8, 64], BF16) for i in range(2)]
    keT = [sb(f"keT{i}", [128, 128], BF16) for i in range(2)]
    sm = [sb(f"sm{i}", [128, 128], BF16) for i in range(2)]
    vm = [sb(f"vm{i}", [128, 8, 64], BF16) for i in range(2)]
    psT = st.enter_context(nc.psum_tensor("g_psT", [128, 128], BF16))
    pss = [st.enter_context(nc.psum_tensor(f"g_pss{i}", [128, 128], F32)) for i in range(2)]
    psd = [st.enter_context(nc.psum_tensor(f"g_psd{i}", [128, 512], F32)) for i in range(2)]
    pso = [st.enter_context(nc.psum_tensor(f"g_pso{i}", [64, 128], F32)) for i in range(2)]

    b.dma(tmask[:], dr["tmask"], w=["tmask"])
    b.dma(cmask[:], dr["cmask"], w=["cmask"])
    b.dma(ident[:], dr["ident"], w=["ident"])
    b.memset(smask[:], 1.0, w=["smask"])
    b.memset(smask[:, :].rearrange("p (c k) -> p c k", k=16)[:, :, 0:1], 0.0, w=["smask"])
    b.memset(onesc[:], 1.0, w=["onesc"])
    it = 0
    nblk = 0
    for h in range(8):
        hv = h % 2
        b.dma(vv[hv][:, :, :], dr["vb"][:, h * 64:(h + 1) * 64].rearrange("(t p) d -> p t d", p=128), w=[("vv", hv)], q="act")
        for d in range(2):
            i = it % 2
            it += 1
            b.dma(lf[i][:, :], dr["lf"][d, h], w=[("lf", i)])
            b.dma(ky[i][:, :], dr["ky"][d, h], w=[("ky", i)])
            b.dma(qq[i][:, :], dr["qb"][h], w=[("qq", i)])
            for c0 in range(0, T, 512):
                b.S.add("dve", lambda o=bb[:, c0:c0 + 512], m=smask[:, :], x=lf[i][:, c0:c0 + 512]:
                        nc.vector.tensor_tensor_scan(out=o, data0=m, data1=x, initial=0.0, op0=ALU.mult, op1=ALU.add),
                        [("lf", i), "smask"], ["bb"])
            b3 = bb[:, :].rearrange("p (c k) -> p c k", k=16)
            tot = b3[:, :, 15]
            b.S.add("dve", lambda: nc.vector.tensor_tensor_scan(out=cum[:, :], data0=onesc[:, :], data1=tot, initial=0.0,
                                                                op0=ALU.mult, op1=ALU.add), ["bb", "onesc"], ["cum"])
            b.act(dec[:, :], tot, AF.Exp, r=["bb"], w=["dec"])
            b.act(dtot[:, d, h:h + 1], cum[:, NC - 1:NC], AF.Exp, r=["cum"], w=["dtot"])
            if d == 0:
                b.tt(pex[:, :], cum[:, :], tot, ALU.subtract, r=["cum", "bb"], w=["pex"])
            else:
                b.ts(pex[:, :], cum[:, :], -1.0, cum[:, NC - 1:NC], ALU.mult, ALU.add, r=["cum"], w=["pex"])
                b.tt(e1[:, :], lf[i][:, :], bb[:, :], ALU.subtract, r=[("lf", i), "bb"], w=["e1"])
                b.tt(e2[:, :].rearrange("p (c k) -> p c k", k=16), e1[:, :].rearrange("p (c k) -> p c k", k=16),
                     tot.unsqueeze(2).to_broadcast([128, NC, 16]), ALU.add, r=["e1", "bb"], w=["e2"])
                b.cp(bb[:, :], e2[:, :], r=["e2"], w=["bb"], eng="pool")
            b.act(pex[:, :], pex[:, :], AF.Exp, r=["pex"], w=["pex"])
            b.act(e1[:, :], bb[:, :], AF.Exp, r=["bb"], w=["e1"])
            b.act(e2[:, :], bb[:, :], AF.Exp, r=["bb"], w=["e2"], scale=-1.0)
            b.tt(qd[:, :], qq[i][:, :], e1[:, :], ALU.mult, r=[("qq", i), "e1"], w=["qd"])
            b.tt(e2[:, :], ky[i][:, :], e2[:, :], ALU.mult, r=[("ky", i), "e2"], w=["e2"])
            b.cp(ki[:, :], e2[:, :], r=["e2"], w=["ki"], eng="pool")
            b.tt(ke[:, :].rearrange("p (c k) -> p c k", k=16), e2[:, :].rearrange("p (c k) -> p c k", k=16),
                 dec[:, :].unsqueeze(2).to_broadcast([128, NC, 16]), ALU.mult, r=["e2", "dec"], w=["ke"])
            b.tt(qp[:, :].rearrange("p (c k) -> p c k", k=16), qd[:, :].rearrange("p (c k) -> p c k", k=16),
                 pex[:, :].unsqueeze(2).to_broadcast([128, NC, 16]), ALU.mult, r=["qd", "pex"], w=["qp"], eng="pool")
            b.dma(dr["qP"][d, h], qp[:, :], r=["qp"], q="pool")
            b.memset(stf[:, nblk % 2, 0, :], 0.0, w=[("stf", nblk % 2)], eng="dve")
            blocks = range(NB) if d == 0 else range(NB - 1, -1, -1)
            for bi, blk in enumerate(blocks):
                cur = nblk % 2
                nxt = 1 - cur
                nblk += 1
                t0 = blk * 128
                b.tr(psT[:, :], ke[:, t0:t0 + 128], ident[:, :], r=["ke", "ident"], w=[("ps", "gT")])
                b.cp(keT[cur][:, :], psT[:, :], r=[("ps", "gT")], w=[("keT", cur)], eng="act")
                b.mm(pss[cur][:, :], ki[:, t0:t0 + 128], qd[:, t0:t0 + 128], True, True, r=["ki", "qd"], w=[("ps", "gs", cur)])
                b.tt(sm[cur][:, :], pss[cur][:, :], tmask[:, d, :], ALU.mult, r=[("ps", "gs", cur), "tmask"], w=[("sm", cur)])
                b.tt(vm[cur][:, :, :], vv[hv][:, blk, :].unsqueeze(1).to_broadcast([128, 8, 64]),
                     cmask[:, :].unsqueeze(2).to_broadcast([128, 8, 64]), ALU.mult, r=[("vv", hv), "cmask"], w=[("vm", cur)], eng="pool")
                b.mm(psd[cur][:, :], keT[cur][:, :], vm[cur][:, :, :].rearrange("p j d -> p (j d)"), True, True,
                     r=[("keT", cur), ("vm", cur)], w=[("ps", "gd", cur)])
                order = range(8) if d == 0 else range(7, -1, -1)
                for si, j in enumerate(order):
                    cidx = blk * 8 + j
                    src = stf[:, cur, si, :]
                    dst = stf[:, cur, si + 1, :] if si < 7 else stf[:, nxt, 0, :]
                    wk = [("stf", cur)] if si < 7 else [("stf", nxt)]
                    b.stt(dst, src, dec[:, cidx:cidx + 1], psd[cur][:, j * 64:(j + 1) * 64], ALU.mult, ALU.add,
                          r=[("stf", cur), "dec", ("ps", "gd", cur)], w=wk)
                b.cp(st16[cur][:, :, :], stf[:, cur, :, :], r=[("stf", cur)], w=[("st16", cur)], eng="act")
                b.mm(pso[cur][:, :], vv[hv][:, blk, :], sm[cur][:, :], True, False, r=[("vv", hv), ("sm", cur)], w=[("ps", "go", cur)])
                for si, j in enumerate(order):
                    b.mm(pso[cur][:, j * 16:(j + 1) * 16], st16[cur][:, si, :], qd[:, t0 + j * 16:t0 + (j + 1) * 16], False, si == 7,
                         r=[("st16", cur), "qd"], w=[("ps", "go", cur)])
                b.cp(osb[:, t0:t0 + 128], pso[cur][:, :], r=[("ps", "go", cur)], w=["osb"], eng="act")
            b.dma(dr["oh"][d, h * 64:(h + 1) * 64, :], osb[:, :], r=["osb"], q="pool")
            b.dma(dr["Sloc"][d, h], stf[:, nblk % 2, 0, :], r=[("stf", nblk % 2)], q="pool")
    b.dma(dr["Dtot"], dtot[:, :, :], r=["dtot"], q="pool")


def hgrn_consts():
    s = np.arange(128)[:, None]
    t = np.arange(128)[None, :]
    same = (s // 16) == (t // 16)
    tm = np.stack([(same & (s <= t)), (same & (s >= t))], 1).astype(np.float32)
    cm = ((np.arange(128)[:, None] // 16) == np.arange(8)[None, :]).astype(np.float32).astype(ml_dtypes.bfloat16)
    return {"tmask": np.ascontiguousarray(tm), "cmask": cm,
            "ident": np.eye(128, dtype=np.float32).astype(ml_dtypes.bfloat16)}


def phase_hgrn2(nc, S, b, T, dr, st):
    sb = lambda name, shape, dt: st.enter_context(nc.sbuf_tensor("f_" + name, shape, dt))
    NTB = T // 512
    Dall = sb("Dall", [128, 4, 2, 8], F32)
    sel = sb("sel", [128, 2, 4], F32)
    gout = sb("gout", [64, 1], F32)
    ones = sb("ones", [128, 128], BF16)
    Sall = [sb(f"Sall{i}", [128, 4, 64], F32) for i in range(2)]
    Tc = [sb(f"Tc{i}", [128, 64], F32) for i in range(2)]
    acc = [sb(f"acc{i}", [128, 64], F32) for i in range(2)]
    sin16 = sb("sin16", [128, 2, 8, 64], BF16)
    qpt = [sb(f"qpt{i}", [128, 512], BF16) for i in range(4)]
    oht = [sb(f"oht{i}", [64, 512], F32) for i in range(4)]
    gbt = [sb(f"gbt{i}", [64, 512], BF16) for i in range(2)]
    o1 = sb("o1", [64, 512], F32)
    o2 = sb("o2", [64, 512], F32)
    osq = sb("osq", [64, 512], BF16)
    rt = sb("rt", [64, 512], F32)
    rr = sb("rr", [64, 512], F32)
    y1 = sb("y1", [64, 512], F32)
    yo = [sb(f"yo{i}", [64, 512], BF16) for i in range(2)]
    pc = [st.enter_context(nc.psum_tensor(f"f_pc{i}", [64, 512], F32)) for i in range(2)]
    pq = [st.enter_context(nc.psum_tensor(f"f_pq{i}", [64, 512], F32)) for i in range(2)]
    b.dma(Dall[:], dr["DtotAll"].rearrange("j p d h -> p j d h"), w=["Dall"])
    b.dma(sel[:], dr["hsel"], w=["sel"])
    b.dma(gout[:], dr["b_gout"], w=["gout"])
    b.dma(ones[:], dr["ones"], w=["ones"])
    n = 0
    for d in range(2):
        order = [0, 1, 2, 3] if d == 0 else [3, 2, 1, 0]
        for h in range(8):
            i = n % 2
            n += 1
            b.dma(Sall[i][:], dr["SlocAll"][:, d, h].rearrange("j p v -> p j v"), w=[("Sall", i)])
            for si, j in enumerate(order):
                if si == 0:
                    b.cp(Tc[i][:], Sall[i][:, j, :], r=[("Sall", i)], w=[("Tc", i)])
                    b.ts(acc[i][:], Tc[i][:], sel[:, d, j:j + 1], None, ALU.mult, r=[("Tc", i), "sel"], w=[("acc", i)])
                else:
                    b.stt(Tc[i][:], Tc[i][:], Dall[:, j, d, h:h + 1], Sall[i][:, j, :], ALU.mult, ALU.add,
                          r=[("Tc", i), "Dall", ("Sall", i)], w=[("Tc", i)])
                    b.stt(acc[i][:], Tc[i][:], sel[:, d, j:j + 1], acc[i][:], ALU.mult, ALU.add,
                          r=[("Tc", i), "sel", ("acc", i)], w=[("acc", i)])
            b.cp(sin16[:, d, h, :], acc[i][:], r=[("acc", i)], w=["sin16"], eng="act")
    n = 0
    for h in range(8):
        for tb in range(NTB):
            tsl = slice(tb * 512, (tb + 1) * 512)
            k = n % 2
            n += 1
            for d in range(2):
                b.dma(qpt[2 * k + d][:], dr["qP"][d, h, :, tsl], w=[("qpt", 2 * k + d)])
                b.dma(oht[2 * k + d][:], dr["oh"][d, h * 64:(h + 1) * 64, tsl], w=[("oht", 2 * k + d)], q="act")
            b.dma(gbt[k][:], dr["gb"][h * 64:(h + 1) * 64, tsl], w=[("gbt", k)], q="act")
            for d in range(2):
                b.mm(pc[k][:, :], sin16[:, d, h, :], qpt[2 * k + d][:, :], d == 0, d == 1, r=["sin16", ("qpt", 2 * k + d)], w=[("ps", "fc", k)])
            b.tt(o1[:], pc[k][:, :], oht[2 * k][:], ALU.add, r=[("ps", "fc", k), ("oht", 2 * k)], w=["o1"])
            b.tt(o2[:], o1[:], oht[2 * k + 1][:], ALU.add, r=["o1", ("oht", 2 * k + 1)], w=["o2"])
            b.act(osq[:], o2[:], AF.Square, r=["o2"], w=["osq"])
            b.mm(pq[k][:, :], ones[0:64, 0:64], osq[:, :], True, True, r=["ones", "osq"], w=[("ps", "fq", k)])
            b.act(rt[:], pq[k][:, :], AF.Sqrt, r=[("ps", "fq", k)], w=["rt"], scale=1.0 / 64, bias=EPS)
            b.recip(rr[:], rt[:], r=["rt"], w=["rr"])
            b.stt(y1[:], o2[:], gout[:, 0:1], rr[:], ALU.mult, ALU.mult, r=["o2", "gout", "rr"], w=["y1"])
            b.tt(yo[k][:], y1[:], gbt[k][:], ALU.mult, r=["y1", ("gbt", k)], w=[("yo", k)], eng="pool")
            b.dma(dr["yb"][h * 64:(h + 1) * 64, tsl], yo[k][:], r=[("yo", k)], q="pool")


def tm_norm_T(b, xt, gbc, hb, psT, ident, dst, kx, kh, kd, junk, ssq, rs1, rs2, eng_cp="act"):
    b.act(junk[:], xt, AF.Square, r=[kx], w=["junk", "ssq"], accum_out=ssq[:])
    b.act(rs1[:], ssq[:], AF.Sqrt, r=["ssq"], w=["rs1"], scale=1.0 / D, bias=EPS)
    b.recip(rs2[:], rs1[:], r=["rs1"], w=["rs2"])
    b.stt(hb[:], xt, rs2[:, 0:1], gbc[:], ALU.mult, ALU.mult, r=[kx, "rs2", "gbc"], w=[kh])
    for j in range(8):
        b.tr(psT[:, j, :], hb[:, j * 128:(j + 1) * 128], ident[:], r=[kh, "ident"], w=[("ps", "T")])
    b.cp(dst, psT[:, :, :], r=[("ps", "T")], w=[kd], eng=eng_cp)


def phase_post1(nc, S, b, T, dr, st):
    sb = lambda name, shape, dt: st.enter_context(nc.sbuf_tensor("p_" + name, shape, dt))
    NTB = T // 512
    ident = sb("ident", [128, 128], BF16)
    ones = sb("ones", [128, 128], BF16)
    gbc = sb("gbc", [128, D], F32)
    wq = sb("wq", [128, 8, 1024], BF16)
    wo = sb("wo", [128, 8, 1024], BF16)
    wout = sb("wout", [128, 8, 1024], BF16)
    wbr = [sb(f"wbr{i}", [128, 4, 1024], BF16) for i in range(3)]
    KxT = sb("KxT", [128, 4, 2, 256], BF16)
    Vx = sb("Vx", [128, 2, 1024], BF16)
    junk = sb("junk", [128, D], BF16)
    ssq = sb("ssq", [128, 1], F32)
    rs1 = sb("rs1", [128, 1], F32)
    rs2 = sb("rs2", [128, 1], F32)
    ps = [st.enter_context(nc.psum_tensor(f"p_ps{i}", [128, 512], F32)) for i in range(7)]
    psT = st.enter_context(nc.psum_tensor("p_psT", [128, 8, 128], BF16))
    wr = lambda ap: ap.rearrange("(kc p) n -> p kc n", p=128)
    b.dma(ident[:], dr["ident"], w=["ident"])
    b.dma(ones[:], dr["ones"], w=["ones"])
    b.dma(wq[:], wr(dr["x_wq"]), w=["wq"], q="pool")
    b.dma(wo[:], wr(dr["x_wo"]), w=["wo"], q="pool")
    b.dma(wout[:], wr(dr["w_out"]), w=["wout"], q="pool")
    for i, nm in enumerate(("w_br_a", "w_br_b", "w_br_c")):
        b.dma(wbr[i][:], wr(dr[nm]), w=[("wbr", i)], q="pool")
    with contextlib.ExitStack() as st2:
        sb2 = lambda name, shape, dt: st2.enter_context(nc.sbuf_tensor("p_" + name, shape, dt))
        wkv = sb2("wkv", [128, 8, 2048], BF16)
        memT = sb2("memT", [128, 8, 256], BF16)
        mt_ = sb2("mt", [128, D], F32)
        hbm = sb2("hbm", [128, D], BF16)
        b.dma(wkv[:], wr(dr["x_wkv"]), w=["wkv"], q="pool")
        b.dma(gbc[:], dr["g_mem"].partition_broadcast(128), w=["gbc"])
        for m in range(2):
            b.dma(mt_[:], dr["mem"][m * 128:(m + 1) * 128, :], w=["mt"])
            tm_norm_T(b, mt_[:], gbc, hbm, psT, ident, memT[:, :, m * 128:(m + 1) * 128], "mt", "hbm", "memT", junk, ssq, rs1, rs2)
        n = 0
        for hh in range(4):
            for dc in range(2):
                bank = n % 4
                n += 1
                c0 = hh * 256 + dc * 128
                for kc in range(8):
                    b.mm(ps[bank][:, 0:256], wkv[:, kc, c0:c0 + 128], memT[:, kc, :], kc == 0, kc == 7, r=["wkv", "memT"], w=[("ps", bank)])
                b.cp(KxT[:, hh, dc, :], ps[bank][:, 0:256], r=[("ps", bank)], w=["KxT"], eng="act" if n % 2 else "dve")
        for m in range(2):
            for half in range(2):
                bank = n % 4
                n += 1
                for kc in range(8):
                    b.mm(ps[bank][:, :], memT[:, kc, m * 128:(m + 1) * 128], wkv[:, kc, 1024 + half * 512:1024 + (half + 1) * 512],
                         kc == 0, kc == 7, r=["wkv", "memT"], w=[("ps", bank)])
                b.cp(Vx[:, m, half * 512:(half + 1) * 512], ps[bank][:, :], r=[("ps", bank)], w=["Vx"], eng="act" if n % 2 else "dve")
        S.flush()
    b.dma(gbc[:], dr["g_x"].partition_broadcast(128), w=["gbc"])
    yT = [sb(f"yT{i}", [128, 4, 512], BF16) for i in range(3)]
    sgt = sb("sgt", [128, 24, 512], BF16)
    mT = sb("mT", [128, 8, 512], BF16)
    t1 = sb("t1", [128, 512], F32)
    macc = sb("macc", [128, 512], F32)
    xt = [sb(f"xt{i}", [128, D], F32) for i in range(4)]
    hb = sb("hb", [128, D], BF16)
    h2T = sb("h2T", [128, 8, 512], BF16)
    qx = sb("qx", [128, 8, 512], BF16)
    pT = sb("pT", [128, 2, 512], BF16)
    rec = sb("rec", [128, 512], F32)
    oxT = sb("oxT", [128, 8, 512], BF16)
    nb = 0

    def bank():
        nonlocal nb
        v = nb % 7
        nb += 1
        return v

    for tb in range(NTB):
        tsl = slice(tb * 512, (tb + 1) * 512)
        for i, nm in enumerate(("ya", "yb", "yc")):
            b.dma(yT[i][:], dr[nm][:, tsl].rearrange("(kc p) t -> p kc t", p=128), w=[("yT", i)], q="act" if i % 2 else "sp")
        b.dma(sgt[:], dr["sg"][:, tsl].rearrange("(c p) t -> p c t", p=128), w=["sgt"])
        for sub in range(4):
            b.dma(xt[sub][:], dr["x"][tb * 512 + sub * 128: tb * 512 + (sub + 1) * 128, :], w=[("xt", sub)], q="act")
        for c in range(8):
            for br in range(3):
                bk = bank()
                for kc in range(4):
                    b.mm(ps[bk][:, :], wbr[br][:, kc, c * 128:(c + 1) * 128], yT[br][:, kc, :], kc == 0, kc == 3,
                         r=[("wbr", br), ("yT", br)], w=[("ps", bk)])
                if br == 0:
                    b.tt(macc[:], ps[bk][:, :], sgt[:, br * 8 + c, :], ALU.mult, r=[("ps", bk), "sgt"], w=["macc"])
                else:
                    b.tt(t1[:], ps[bk][:, :], sgt[:, br * 8 + c, :], ALU.mult, r=[("ps", bk), "sgt"], w=["t1"])
                    if br == 1:
                        b.tt(macc[:], macc[:], t1[:], ALU.add, r=["macc", "t1"], w=["macc"], eng="pool")
                    else:
                        b.tt(mT[:, c, :], macc[:], t1[:], ALU.add, r=["macc", "t1"], w=["mT"], eng="pool")
        for sub in range(4):
            for half in range(2):
                bk = bank()
                for c in range(8):
                    b.mm(ps[bk][:, :], mT[:, c, sub * 128:(sub + 1) * 128], wout[:, c, half * 512:(half + 1) * 512], c == 0, c == 7,
                         r=["mT", "wout"], w=[("ps", bk)])
                b.tt(xt[sub][:, half * 512:(half + 1) * 512], ps[bk][:, :], xt[sub][:, half * 512:(half + 1) * 512], ALU.add,
                     r=[("ps", bk), ("xt", sub)], w=[("xt", sub)])
            tm_norm_T(b, xt[sub][:], gbc, hb, psT, ident, h2T[:, :, sub * 128:(sub + 1) * 128], ("xt", sub), "hb", "h2T", junk, ssq, rs1, rs2)
        for hh in range(4):
            for dc in range(2):
                bk = bank()
                c0 = hh * 256 + dc * 128
                for kc in range(8):
                    b.mm(ps[bk][:, :], wq[:, kc, c0:c0 + 128], h2T[:, kc, :], kc == 0, kc == 7, r=["wq", "h2T"], w=[("ps", bk)])
                b.cp(qx[:, hh * 2 + dc, :], ps[bk][:, :], r=[("ps", bk)], w=[("qx", hh)], eng="act" if dc else "dve")
        for hh in range(4):
            for m in range(2):
                bk = bank()
                for dc in range(2):
                    b.mm(ps[bk][:, :], KxT[:, hh, dc, m * 128:(m + 1) * 128], qx[:, hh * 2 + dc, :], dc == 0, dc == 1,
                         r=["KxT", ("qx", hh)], w=[("ps", bk)])
                b.act(pT[:, m, :], ps[bk][:, :], AF.Exp, r=[("ps", bk)], w=[("pT", m)], scale=1.0 / 16)
            bk = bank()
            for m in range(2):
                b.mm(ps[bk][:, :], ones[:, :], pT[:, m, :], m == 0, m == 1, r=["ones", ("pT", m)], w=[("ps", bk)])
            b.recip(rec[:], ps[bk][:, :], r=[("ps", bk)], w=["rec"])
            for vc in range(2):
                bk = bank()
                for m in range(2):
                    b.mm(ps[bk][:, :], Vx[:, m, hh * 256 + vc * 128: hh * 256 + (vc + 1) * 128], pT[:, m, :], m == 0, m == 1,
                         r=["Vx", ("pT", m)], w=[("ps", bk)])
                b.tt(oxT[:, hh * 2 + vc, :], ps[bk][:, :], rec[:], ALU.mult, r=[("ps", bk), "rec"], w=["oxT"])
        for sub in range(4):
            o = (tb * 4 + sub) % 2
            for half in range(2):
                bk = bank()
                for c in range(8):
                    b.mm(ps[bk][:, :], oxT[:, c, sub * 128:(sub + 1) * 128], wo[:, c, half * 512:(half + 1) * 512], c == 0, c == 7,
                         r=["oxT", "wo"], w=[("ps", bk)])
                b.tt(xt[sub][:, half * 512:(half + 1) * 512], ps[bk][:, :], xt[sub][:, half * 512:(half + 1) * 512], ALU.add,
                     r=[("ps", bk), ("xt", sub)], w=[("xt", sub)])
            b.dma(dr["x2"][tb * 512 + sub * 128: tb * 512 + (sub + 1) * 128, :], xt[sub][:], r=[("xt", sub)], w=[("x2", tb * 4 + sub)], q="pool")


def phase_ffn(nc, S, b, T, dr, st, final):
    sb = lambda name, shape, dt: st.enter_context(nc.sbuf_tensor("n_" + name, shape, dt))
    TBF = 256
    NTB = T // TBF
    NF = DFF // 128
    ident = sb("ident", [128, 128], BF16)
    gbc = sb("gbc", [128, D], F32)
    gfin = sb("gfin", [128, D], F32)
    w1 = sb("w1", [128, 8, DFF], BF16)
    w3 = sb("w3", [128, 8, DFF], BF16)
    w2 = sb("w2", [128, NF, D], BF16)
    junk = sb("junk", [128, D], BF16)
    ssq = sb("ssq", [128, 1], F32)
    rs1 = sb("rs1", [128, 1], F32)
    rs2 = sb("rs2", [128, 1], F32)
    xt = [sb(f"xt{i}", [128, D], F32) for i in range(2)]
    hb = sb("hb", [128, D], BF16)
    h3T = sb("h3T", [128, 8, TBF], BF16)
    aT = sb("aT", [128, NF, TBF], BF16)
    su = [sb(f"su{i}", [128, TBF], F32) for i in range(2)]
    xo = [sb(f"xo{i}", [128, D], F32) for i in range(2)]
    ps = [st.enter_context(nc.psum_tensor(f"n_ps{i}", [128, 512], F32)) for i in range(7)]
    psT = st.enter_context(nc.psum_tensor("n_psT", [128, 8, 128], BF16))
    wr = lambda ap: ap.rearrange("(kc p) n -> p kc n", p=128)
    b.dma(ident[:], dr["ident"], w=["ident"])
    b.dma(gbc[:], dr["g_ffn"].partition_broadcast(128), w=["gbc"])
    if final:
        b.dma(gfin[:], dr["g_final"].partition_broadcast(128), w=["gfin"])
    for half in range(2):
        hs = slice(half * 1408, (half + 1) * 1408)
        b.dma(w1[:, :, hs], wr(dr["f_w1"])[:, :, hs], w=["w1"], q="pool")
        b.dma(w3[:, :, hs], wr(dr["f_w3"])[:, :, hs], w=["w3"], q="pool")
        b.dma(w2[:, half * 11:(half + 1) * 11, :], wr(dr["f_w2"])[:, half * 11:(half + 1) * 11, :], w=["w2"], q="pool")
    nb = 0

    def bank():
        nonlocal nb
        v = nb % 7
        nb += 1
        return v

    for tb in range(NTB):
        for sub in range(2):
            r0 = tb * TBF + sub * 128
            b.dma(xt[sub][:], dr["x2"][r0:r0 + 128, :], r=[("x2", r0 // 128)], w=[("xt", sub)])
            tm_norm_T(b, xt[sub][:], gbc, hb, psT, ident, h3T[:, :, sub * 128:(sub + 1) * 128], ("xt", sub), "hb", "h3T", junk, ssq, rs1, rs2)
        for fc in range(NF):
            bu, bv = bank(), bank()
            for kc in range(8):
                b.mm(ps[bu][:, 0:TBF], w1[:, kc, fc * 128:(fc + 1) * 128], h3T[:, kc, :], kc == 0, kc == 7, r=["w1", "h3T"], w=[("ps", bu)])
            for kc in range(8):
                b.mm(ps[bv][:, 0:TBF], w3[:, kc, fc * 128:(fc + 1) * 128], h3T[:, kc, :], kc == 0, kc == 7, r=["w3", "h3T"], w=[("ps", bv)])
            k = fc % 2
            b.act(su[k][:], ps[bu][:, 0:TBF], AF.Silu, r=[("ps", bu)], w=[("su", k)])
            b.tt(aT[:, fc, :], ps[bv][:, 0:TBF], su[k][:], ALU.mult, r=[("ps", bv), ("su", k)], w=["aT"])
        for sub in range(2):
            r0 = tb * TBF + sub * 128
            o = sub
            for half in range(2):
                bk = bank()
                for fc in range(NF):
                    b.mm(ps[bk][:, :], aT[:, fc, sub * 128:(sub + 1) * 128], w2[:, fc, half * 512:(half + 1) * 512], fc == 0, fc == NF - 1,
                         r=["aT", "w2"], w=[("ps", bk)])
                b.tt(xo[o][:, half * 512:(half + 1) * 512], ps[bk][:, :], xt[sub][:, half * 512:(half + 1) * 512], ALU.add,
                     r=[("ps", bk), ("xt", sub)], w=[("xo", o)])
            if final:
                b.act(junk[:], xo[o][:], AF.Square, r=[("xo", o)], w=["junk", "ssq"], accum_out=ssq[:])
                b.act(rs1[:], ssq[:], AF.Sqrt, r=["ssq"], w=["rs1"], scale=1.0 / D, bias=EPS)
                b.recip(rs2[:], rs1[:], r=["rs1"], w=["rs2"])
                b.stt(xo[o][:], xo[o][:], rs2[:, 0:1], gfin[:], ALU.mult, ALU.mult, r=[("xo", o), "rs2", "gfin"], w=[("xo", o)])
            b.dma(dr["xout"][r0:r0 + 128, :], xo[o][:], r=[("xo", o)], q="pool")


def build_A(T):
    nc = bass.Bass("TRN2", target_bir_lowering=False)
    dr = {}
    for k, s in A_INPUTS.items():
        dr[k] = dram_in(nc, k, _shape(s, T), BF16 if k == "ones" else F32)
    for k, shp, dt in (("ident", [128, 128], BF16), ("tmask", [128, 2, 128], F32), ("cmask", [128, 8], BF16)):
        dr[k] = dram_in(nc, k, shp, dt)
    for k, (s, dt) in A_OUTPUTS.items():
        dr[k] = dram_out(nc, k, _shape(s, T), dt)
    for k, shp, dt in (("oh", [2, 512, T], F32), ("qP", [2, 8, 128, T], BF16), ("Sloc", [2, 8, 128, 64], F32), ("Dtot", [128, 2, 8], F32)):
        dr[k] = dram_out(nc, k, shp, dt)
    with contextlib.ExitStack() as st:
        S = Sched(nc, st)
        b = B(nc, S)
        with contextlib.ExitStack() as st2:
            phase_inproj(nc, S, b, T, dr, st2)
            S.flush()
        with contextlib.ExitStack() as st2:
            phase_hgrn1(nc, S, b, T, dr, st2)
            S.flush()
    return nc


B_INPUTS = lambda T, SEQ: [
    ("x", [T, D], F32), ("qa", [8, 96, T], BF16), ("kaF", [8, 96, SEQ], BF16), ("vaF", [SEQ, 512], BF16),
    ("qc", [512, T], BF16), ("kcH", [128, T + 256], BF16), ("vcH", [T + 256, 128], BF16),
    ("wbias", [128, 3, 8, 128], F32), ("wflag", [128, 2], F32), ("c_sink", [128, 8], F32),
    ("qP", [2, 8, 128, T], BF16), ("oh", [2, 512, T], F32), ("gb", [512, T], BF16), ("sg", [3072, T], BF16),
    ("SlocAll", [4, 2, 8, 128, 64], F32), ("DtotAll", [4, 128, 2, 8], F32), ("hsel", [128, 2, 4], F32),
    ("b_gout", [64, 1], F32), ("ones", [128, 128], BF16), ("ident", [128, 128], BF16),
    ("w_br_a", [512, D], F32), ("w_br_b", [512, D], F32), ("w_br_c", [512, D], F32), ("w_out", [D, D], F32),
    ("g_x", [D], F32), ("g_mem", [D], F32), ("mem", [256, D], F32), ("x_wq", [D, D], F32), ("x_wkv", [D, 2 * D], F32),
    ("x_wo", [D, D], F32), ("g_ffn", [D], F32), ("f_w1", [D, DFF], F32), ("f_w3", [D, DFF], F32), ("f_w2", [DFF, D], F32),
    ("g_final", [D], F32),
]


def build_B(T, SEQ, final, upto=99):
    nc = bass.Bass("TRN2", target_bir_lowering=False)
    dr = {}
    for k, shp, dt in B_INPUTS(T, SEQ):
        dr[k] = dram_in(nc, k, shp, dt)
    dr["xout"] = dram_out(nc, "xout", [T, D], F32)
    for k in ("ya", "yb", "yc"):
        dr[k] = nc.dram_tensor(k, [512, T], BF16, kind="ExternalOutput" if upto < 99 else "Internal").ap()
    dr["x2"] = nc.dram_tensor("x2", [T, D], F32, kind="ExternalOutput" if upto < 99 else "Internal").ap()
    with contextlib.ExitStack() as st:
        S = Sched(nc, st)
        b = B(nc, S)
        phases = [lambda s2: phase_mla(nc, S, b, T, SEQ, dr, s2), lambda s2: phase_win(nc, S, b, T, dr, s2),
                  lambda s2: phase_hgrn2(nc, S, b, T, dr, s2), lambda s2: phase_post1(nc, S, b, T, dr, s2),
                  lambda s2: phase_ffn(nc, S, b, T, dr, s2, final)]
        for i, ph in enumerate(phases):
            if i >= upto:
                break
            with contextlib.ExitStack() as st2:
                ph(st2)
                S.flush()
    return nc


def layer_consts_B(inp, l):
    c = lambda a: np.ascontiguousarray(a)
    return {
        "wbias": win_bias_table(inp["rel_bias"]), "c_sink": c(np.tile(inp["c_sink"][l][None], (128, 1))),
        "b_gout": c(inp["b_gout"][l].reshape(64, 1)),
        "ones": np.ones((128, 128), np.float32).astype(ml_dtypes.bfloat16),
        "ident": np.eye(128, dtype=np.float32).astype(ml_dtypes.bfloat16),
        "w_br_a": c(inp["w_br_a"][l]), "w_br_b": c(inp["w_br_b"][l]), "w_br_c": c(inp["w_br_c"][l]), "w_out": c(inp["w_out"][l]),
        "g_x": c(inp["g_x"][l]), "g_mem": c(inp["g_mem"][l]), "x_wq": c(inp["x_wq"][l]), "x_wkv": c(inp["x_wkv"][l]),
        "x_wo": c(inp["x_wo"][l]), "g_ffn": c(inp["g_ffn"][l]), "f_w1": c(inp["f_w1"][l]), "f_w3": c(inp["f_w3"][l]),
        "f_w2": c(inp["f_w2"][l]), "g_final": c(inp["g_final"]),
    }


_PROGS = {}


def _prog(key, fn):
    if key not in _PROGS:
        _PROGS[key] = fn()
    return _PROGS[key]


def run_model(inp, NR=4, runner=None):
    if runner is None:
        runner = lambda nc, maps: run_bass_kernel_spmd(nc, maps, core_ids=list(range(len(maps)))).results
    inp = {k: np.asarray(v) for k, v in inp.items()}
    Bsz, SEQ, _ = inp["x"].shape
    T = SEQ // NR
    ncore = Bsz * NR
    depth = inp["w_in"].shape[0]
    xs = [np.ascontiguousarray(inp["x"][c // NR, (c % NR) * T:(c % NR + 1) * T]) for c in range(ncore)]
    hc = hgrn_consts()
    zero_k = np.zeros((128, 128), ml_dtypes.bfloat16)
    for l in range(depth):
        cA = layer_consts_A(inp, l)
        cA.update(hc)
        mapsA = []
        for c in range(ncore):
            r = c % NR
            C, Sn = rope_tables(np.arange(r * T, (r + 1) * T))
            m = dict(cA)
            m.update({"x": xs[c], "ropeC": C, "ropeS": Sn})
            mapsA.append(m)
        ncA = _prog(("A", T), lambda: build_A(T))
        ra = runner(ncA, mapsA)
        cB = layer_consts_B(inp, l)
        mapsB = []
        for c in range(ncore):
            bi, r = c // NR, c % NR
            grp = [ra[bi * NR + j] for j in range(NR)]
            kaF = np.concatenate([g["ka"] for g in grp], axis=2)
            vaF = np.concatenate([g["va"] for g in grp], axis=0)
            kl = grp[r - 1]["kc"][:, -128:] if r > 0 else zero_k
            kr_ = grp[r + 1]["kc"][:, :128] if r < NR - 1 else zero_k
            vl = grp[r - 1]["vc"][-128:] if r > 0 else zero_k
            vr = grp[r + 1]["vc"][:128] if r < NR - 1 else zero_k
            hsel = np.zeros((128, 2, 4), np.float32)
            if r > 0:
                hsel[:, 0, r - 1] = 1.0
            if r < NR - 1:
                hsel[:, 1, r + 1] = 1.0
            SlocAll = np.zeros((4, 2, 8, 128, 64), np.float32)
            DtotAll = np.zeros((4, 128, 2, 8), np.float32)
            for j in range(NR):
                SlocAll[j] = grp[j]["Sloc"]
                DtotAll[j] = grp[j]["Dtot"]
            wflag = np.zeros((128, 2), np.float32)
            wflag[:, 0] = 1.0 if r > 0 else 0.0
            wflag[:, 1] = 1.0 if r < NR - 1 else 0.0
            me = ra[c]
            m = dict(cB)
            m.update({"x": xs[c], "qa": me["qa"], "kaF": kaF, "vaF": vaF, "qc": me["qc"],
                      "kcH": np.ascontiguousarray(np.concatenate([kl, me["kc"], kr_], axis=1)),
                      "vcH": np.ascontiguousarray(np.concatenate([vl, me["vc"], vr], axis=0)),
                      "wflag": wflag, "qP": me["qP"], "oh": me["oh"], "gb": me["gb"], "sg": me["sg"],
                      "SlocAll": SlocAll, "DtotAll": DtotAll, "hsel": hsel,
                      "mem": np.ascontiguousarray(inp["mem"][bi])})
            mapsB.append(m)
        final = (l == depth - 1)
        ncB = _prog(("B", T, SEQ, final), lambda: build_B(T, SEQ, final))
        rb = runner(ncB, mapsB)
        xs = [np.asarray(rb[c]["xout"]) for c in range(ncore)]
    out = np.zeros((Bsz, SEQ, D), np.float32)
    for c in range(ncore):
        out[c // NR, (c % NR) * T:(c % NR + 1) * T] = xs[c]
    return out


def kernel(**inputs):
    return run_model(inputs, NR=4)
```
